# Optimizing a Trainium2 kernel written in Bass

```python
import math
import jax
import jax.numpy as jnp
from jax import lax
import numpy as np


D_MODEL = 1024
BATCH = 1
SEQ = 16384
DEPTH = 2

GRID_W = 64
CTX_LEN = 256
N_MIXERS = 2
N_GLA = (DEPTH + N_MIXERS - 1) // N_MIXERS
N_DIFF = DEPTH // N_MIXERS
GLA_HEADS = 4
GLA_DK = D_MODEL // 2 // GLA_HEADS
GLA_DV = D_MODEL // GLA_HEADS
GLA_GATE_RANK = 16
GLA_GATE_NORM = 16.0
GLA_CHUNK = 64
DIFF_HEAD_DIM = 64
DIFF_HEADS = D_MODEL // (2 * DIFF_HEAD_DIM)
ROPE_BASE = 10000.0
Q_BLOCK = 128
D_FF = 2816
CONV_WIDTH = 3
N_MOD = 6
EPS = 1e-6

kernel_name = 'hybrid_gla_diffattn_convffn_dit'


def rmsnorm(x, g):
    xf = x.astype(jnp.float32)
    y = xf * lax.rsqrt(jnp.mean(xf * xf, axis=-1, keepdims=True) + EPS)
    return y.astype(x.dtype) * g


def split_heads(t, heads):
    bsz, length, _ = t.shape
    return t.reshape(bsz, length, heads, -1).transpose(0, 2, 1, 3)


def merge_heads(t):
    bsz, heads, length, dim = t.shape
    return t.transpose(0, 2, 1, 3).reshape(bsz, length, heads * dim)


def axial_rope_tables(n_tokens, dtype):
    rows = n_tokens // GRID_W
    row = jnp.repeat(jnp.arange(rows, dtype=jnp.float32), GRID_W)
    col = jnp.tile(jnp.arange(GRID_W, dtype=jnp.float32), rows)
    quarter = DIFF_HEAD_DIM // 4
    inv = ROPE_BASE ** (-jnp.arange(quarter, dtype=jnp.float32) / quarter)
    ang_r = row[:, None] * inv
    ang_c = col[:, None] * inv
    ang = jnp.concatenate([ang_r, ang_r, ang_c, ang_c], axis=-1)
    return jnp.cos(ang).astype(dtype), jnp.sin(ang).astype(dtype)


def apply_axial_rope(x, cos, sin):
    xs = x.reshape(x.shape[:-1] + (2, 2, DIFF_HEAD_DIM // 4))
    rot = jnp.stack([-xs[..., 1, :], xs[..., 0, :]], axis=-2).reshape(x.shape)
    return x * cos + rot * sin


def gla_chunk_scan(q, k, v, log_g, s0):
    bsz, heads, length, _ = q.shape
    n_chunks = length // GLA_CHUNK

    def to_chunks(a):
        return a.reshape(bsz, heads, n_chunks, GLA_CHUNK, a.shape[-1]).transpose(2, 0, 1, 3, 4)

    mask = jnp.tril(jnp.ones((GLA_CHUNK, GLA_CHUNK), dtype=bool))[:, :, None]

    def step(state, inp):
        qb, kb, vb, gb = inp
        b = jnp.cumsum(gb, axis=2)
        rel = jnp.where(mask, b[:, :, :, None, :] - b[:, :, None, :, :], -jnp.inf)
        scores = jnp.einsum('bhtd,bhsd,bhtsd->bhts', qb, kb, jnp.exp(rel))
        out = (jnp.einsum('bhts,bhsv->bhtv', scores, vb)
               + jnp.einsum('bhtd,bhdv->bhtv', qb * jnp.exp(b), state))
        b_last = b[:, :, -1:, :]
        state = (jnp.exp(b_last[:, :, 0, :])[..., None] * state
                 + jnp.einsum('bhsd,bhsv->bhdv', kb * jnp.exp(b_last - b), vb))
        return state, out

    state, out = lax.scan(step, s0, (to_chunks(q), to_chunks(k), to_chunks(v), to_chunks(log_g)))
    out = out.transpose(1, 2, 0, 3, 4).reshape(bsz, heads, length, v.shape[-1])
    return out, state


def gla_mixer(hx, hc, wq, wk, wv, wr, wg1, wg2, bg, gn, wo, need_ctx):
    f32 = jnp.float32

    def project(h):
        q = split_heads(h @ wq, GLA_HEADS).astype(f32) * GLA_DK ** -0.5
        k = split_heads(h @ wk, GLA_HEADS).astype(f32)
        v = split_heads(h @ wv, GLA_HEADS).astype(f32)
        log_g = [split_heads(jax.nn.log_sigmoid(((h @ wg1[d]) @ wg2[d] + bg[d]).astype(f32))
                             / GLA_GATE_NORM, GLA_HEADS) for d in range(2)]
        return q, k, v, log_g

    def bidirectional(q, k, v, log_g, s_fwd, s_bwd):
        o_f, st_f = gla_chunk_scan(q, k, v, log_g[0], s_fwd)
        flip = lambda a: jnp.flip(a, axis=2)
        o_b, st_b = gla_chunk_scan(flip(q), flip(k), flip(v), flip(log_g[1]), s_bwd)
        return o_f + flip(o_b), st_f, st_b

    def readout(h, o):
        o = merge_heads(rmsnorm(o, gn)).astype(h.dtype) * jax.nn.silu(h @ wr)
        return o @ wo

    zero = jnp.zeros((hx.shape[0], GLA_HEADS, GLA_DK, GLA_DV), f32)
    qc, kc, vc, gc = project(hc)
    oc, ctx_fwd, ctx_bwd = bidirectional(qc, kc, vc, gc, zero, zero)
    qx, kx, vx, gx = project(hx)
    ox, _, _ = bidirectional(qx, kx, vx, gx, ctx_fwd, ctx_bwd)
    dx = readout(hx, ox)
    dc = readout(hc, oc) if need_ctx else None
    return dx, dc


def diff_mixer(hx, hc, wq, wk, wv, lq1, lk1, lq2, lk2, subln, wo, lambda_init, need_ctx):
    f32 = jnp.float32
    scale = DIFF_HEAD_DIM ** -0.5
    lam = (jnp.exp(jnp.sum(lq1 * lk1).astype(f32)) - jnp.exp(jnp.sum(lq2 * lk2).astype(f32))
           + lambda_init)

    def qkv(h):
        bsz, length, _ = h.shape
        q = (h @ wq).reshape(bsz, length, DIFF_HEADS, 2, DIFF_HEAD_DIM).transpose(3, 0, 2, 1, 4)
        k = (h @ wk).reshape(bsz, length, DIFF_HEADS, 2, DIFF_HEAD_DIM).transpose(3, 0, 2, 1, 4)
        v = split_heads(h @ wv, DIFF_HEADS)
        return q, k, v

    def diff_attend(q, k, v):
        s = jnp.einsum('nbhqd,nbhkd->nbhqk', q, k).astype(f32) * scale
        p = jax.nn.softmax(s, axis=-1)
        return jnp.einsum('bhqk,bhkv->bhqv', p[0] - lam * p[1], v)

    def readout(o, dtype):
        o = rmsnorm(o, subln) * (1.0 - lambda_init)
        return merge_heads(o).astype(dtype) @ wo

    qc, kc, vc = qkv(hc)
    qx, kx, vx = qkv(hx)
    bsz, n_lat = hx.shape[0], hx.shape[1]
    cos, sin = axial_rope_tables(n_lat, qx.dtype)
    qx = apply_axial_rope(qx, cos, sin)
    kx = apply_axial_rope(kx, cos, sin)
    k_all = jnp.concatenate([kc, kx], axis=3)
    v_all = jnp.concatenate([vc, vx], axis=2).astype(f32)
    n_blocks = n_lat // Q_BLOCK
    q_blocks = qx.reshape(2, bsz, DIFF_HEADS, n_blocks, Q_BLOCK, DIFF_HEAD_DIM).transpose(3, 0, 1, 2, 4, 5)
    o_blocks = lax.map(lambda qb: diff_attend(qb, k_all, v_all), q_blocks)
    ox = o_blocks.transpose(1, 2, 0, 3, 4).reshape(bsz, DIFF_HEADS, n_lat, 2 * DIFF_HEAD_DIM)
    dx = readout(ox, hx.dtype)
    dc = readout(diff_attend(qc, kc, vc.astype(f32)), hc.dtype) if need_ctx else None
    return dx, dc


def conv_ffn(h, w_up, w_conv, b_conv, w_down):
    u = h @ w_up
    up = jnp.pad(u, ((0, 0), (1, 1), (0, 0)))
    u = up[:, :-2] * w_conv[0] + up[:, 1:-1] * w_conv[1] + up[:, 2:] * w_conv[2] + b_conv
    a, b = jnp.split(u, 2, axis=-1)
    return (jax.nn.silu(a) * b) @ w_down


def setup_inputs(seed: int = 0) -> dict:
    key = jax.random.key(seed)
    ks = iter(jax.random.split(key, 40))
    f32 = jnp.float32

    def nrm(shape, fan_in):
        return jax.random.normal(next(ks), shape, f32) * fan_in ** -0.5

    def gain(shape):
        return 1.0 + 0.02 * jax.random.normal(next(ks), shape, f32)

    def small(shape, s):
        return s * jax.random.normal(next(ks), shape, f32)

    D = D_MODEL
    return {
        'x': jax.random.normal(next(ks), (BATCH, SEQ, D), f32),
        'c': jax.random.normal(next(ks), (BATCH, D), f32),
        'ctx': jax.random.normal(next(ks), (BATCH, CTX_LEN, D), f32),
        'c_ctx': jax.random.normal(next(ks), (D,), f32),
        'mod_w': nrm((DEPTH, D, N_MOD * D), D),
        'mod_b': small((DEPTH, N_MOD * D), 0.01),
        'norm_mix': gain((DEPTH, D)),
        'norm_ffn': gain((DEPTH, D)),
        'gla_wq': nrm((N_GLA, D, GLA_HEADS * GLA_DK), D),
        'gla_wk': nrm((N_GLA, D, GLA_HEADS * GLA_DK), D),
        'gla_wv': nrm((N_GLA, D, GLA_HEADS * GLA_DV), D),
        'gla_wr': nrm((N_GLA, D, GLA_HEADS * GLA_DV), D),
        'gla_wg1': nrm((N_GLA, 2, D, GLA_GATE_RANK), D),
        'gla_wg2': nrm((N_GLA, 2, GLA_GATE_RANK, GLA_HEADS * GLA_DK), GLA_GATE_RANK),
        'gla_bg': small((N_GLA, 2, GLA_HEADS * GLA_DK), 0.1),
        'gla_norm': gain((N_GLA, GLA_DV)),
        'gla_wo': nrm((N_GLA, GLA_HEADS * GLA_DV, D), GLA_HEADS * GLA_DV),
        'diff_wq': nrm((N_DIFF, D, 2 * DIFF_HEADS * DIFF_HEAD_DIM), D),
        'diff_wk': nrm((N_DIFF, D, 2 * DIFF_HEADS * DIFF_HEAD_DIM), D),
        'diff_wv': nrm((N_DIFF, D, DIFF_HEADS * 2 * DIFF_HEAD_DIM), D),
        'diff_lq1': small((N_DIFF, DIFF_HEAD_DIM), 0.1),
        'diff_lk1': small((N_DIFF, DIFF_HEAD_DIM), 0.1),
        'diff_lq2': small((N_DIFF, DIFF_HEAD_DIM), 0.1),
        'diff_lk2': small((N_DIFF, DIFF_HEAD_DIM), 0.1),
        'diff_subln': gain((N_DIFF, 2 * DIFF_HEAD_DIM)),
        'diff_wo': nrm((N_DIFF, DIFF_HEADS * 2 * DIFF_HEAD_DIM, D), DIFF_HEADS * 2 * DIFF_HEAD_DIM),
        'ffn_wup': nrm((DEPTH, D, 2 * D_FF), D),
        'ffn_conv': nrm((DEPTH, CONV_WIDTH, 2 * D_FF), CONV_WIDTH),
        'ffn_conv_b': small((DEPTH, 2 * D_FF), 0.01),
        'ffn_wdown': nrm((DEPTH, D_FF, D), D_FF),
        'final_norm': gain((D,)),
    }


def reference(x, c, ctx, c_ctx, mod_w, mod_b, norm_mix, norm_ffn,
              gla_wq, gla_wk, gla_wv, gla_wr, gla_wg1, gla_wg2, gla_bg, gla_norm, gla_wo,
              diff_wq, diff_wk, diff_wv, diff_lq1, diff_lk1, diff_lq2, diff_lk2, diff_subln, diff_wo,
              ffn_wup, ffn_conv, ffn_conv_b, ffn_wdown, final_norm):
    silu_c = jax.nn.silu(c)[:, None, :]
    silu_cc = jax.nn.silu(c_ctx)
    for i in range(DEPTH):
        last = i == DEPTH - 1
        mx = jnp.split(silu_c @ mod_w[i] + mod_b[i], N_MOD, axis=-1)
        mc = jnp.split(silu_cc @ mod_w[i] + mod_b[i], N_MOD, axis=-1)
        hx = rmsnorm(x, norm_mix[i]) * (1 + mx[1]) + mx[0]
        hc = rmsnorm(ctx, norm_mix[i]) * (1 + mc[1]) + mc[0]
        j = i // N_MIXERS
        if i % N_MIXERS == 0:
            dx, dc = gla_mixer(hx, hc, gla_wq[j], gla_wk[j], gla_wv[j], gla_wr[j], gla_wg1[j],
                               gla_wg2[j], gla_bg[j], gla_norm[j], gla_wo[j], not last)
        else:
            lambda_init = 0.8 - 0.6 * math.exp(-0.3 * i)
            dx, dc = diff_mixer(hx, hc, diff_wq[j], diff_wk[j], diff_wv[j], diff_lq1[j], diff_lk1[j],
                                diff_lq2[j], diff_lk2[j], diff_subln[j], diff_wo[j], lambda_init, not last)
        x = x + mx[2] * dx
        x = x + mx[5] * conv_ffn(rmsnorm(x, norm_ffn[i]) * (1 + mx[4]) + mx[3],
                                 ffn_wup[i], ffn_conv[i], ffn_conv_b[i], ffn_wdown[i])
        if not last:
            ctx = ctx + mc[2] * dc
            ctx = ctx + mc[5] * conv_ffn(rmsnorm(ctx, norm_ffn[i]) * (1 + mc[4]) + mc[3],
                                         ffn_wup[i], ffn_conv[i], ffn_conv_b[i], ffn_wdown[i])
    return rmsnorm(x, final_norm)
```

```python
from contextlib import ExitStack

import numpy as np
import concourse.bass as bass
import concourse.mybir as mybir
from concourse.bass_utils import run_bass_kernel_spmd

F32 = mybir.dt.float32
BF16 = mybir.dt.bfloat16
AF = mybir.ActivationFunctionType
ALU = mybir.AluOpType

ENGS = ("pe", "act", "dve", "pool", "sp")
SEM_ROT = 30000


class Buf:
    __slots__ = ("name", "w", "r")

    def __init__(self, name=""):
        self.name = name
        self.w = None
        self.r = []


class Op:
    __slots__ = ("eng", "fn", "deps", "idx", "signal", "sem", "val", "dma", "guard", "inc")

    def __init__(self, eng, fn, dma):
        self.eng = eng
        self.fn = fn
        self.deps = []
        self.idx = -1
        self.signal = False
        self.sem = None
        self.val = 0
        self.dma = dma
        self.guard = None


class Sched:
    def __init__(self, nc, n_dma_sems=24, same_engine_sync=True):
        self.nc = nc
        self.ops = {e: [] for e in ENGS}
        self.n_dma_sems = n_dma_sems
        self.same_engine_sync = same_engine_sync
        self.all_bufs = []

    def buf(self, name=""):
        b = Buf(name)
        self.all_bufs.append(b)
        return b

    def op(self, eng, fn, reads=(), writes=(), dma=False, inc=16):
        o = Op(eng, fn, dma)
        o.inc = inc
        o.idx = len(self.ops[eng])
        deps = []
        for t in reads:
            if t.w is not None:
                deps.append((t.w, True))
        for t in writes:
            if t.w is not None:
                deps.append((t.w, True))
            deps.extend((x, False) for x in t.r)
        seen = set()
        for d, strong in deps:
            if id(d) in seen or d is o:
                continue
            if d.eng == eng and not d.dma:
                if eng in ("pe", "sp") or not self.same_engine_sync:
                    continue
            seen.add(id(d))
            o.deps.append(d)
        for t in reads:
            t.r.append(o)
        for t in writes:
            t.w = o
            t.r = []
        self.ops[eng].append(o)
        return o

    def barrier(self):
        prods = []
        for x in self.all_bufs:
            if x.w is not None:
                prods.append(x.w)
            prods.extend(x.r)
        uniq = {}
        for p in prods:
            if p.dma:
                uniq[id(p)] = p
            else:
                key = p.eng
                if key not in uniq or uniq[key].idx < p.idx:
                    uniq[key] = p
        plist = list(uniq.values())
        for e in ENGS:
            o = Op(e, None, False)
            o.inc = 16
            o.idx = len(self.ops[e])
            o.deps = [p for p in plist if not (p.eng == e and not p.dma)]
            self.ops[e].append(o)
        for x in self.all_bufs:
            x.w = None
            x.r = []
        self.all_bufs = [x for x in self.all_bufs if getattr(x, "keep", True)]

    def emit(self, stack):
        nc = self.nc
        for e in ENGS:
            for o in self.ops[e]:
                for d in o.deps:
                    d.signal = True
        eng_sems = {}
        for e in ENGS:
            n_sig = sum(1 for o in self.ops[e] if o.signal and not o.dma)
            n = max(1, (n_sig + SEM_ROT - 1) // SEM_ROT)
            eng_sems[e] = [stack.enter_context(nc.semaphore(f"s_{e}_{i}")) for i in range(n)]
        dma_sems = {}
        for e in ENGS:
            if any(o.dma for o in self.ops[e]):
                dma_sems[e] = [stack.enter_context(nc.semaphore(f"d_{e}_{i}"))
                               for i in range(self.n_dma_sems)]
        for e in ENGS:
            cnt = 0
            dcnt = 0
            duse = [0] * self.n_dma_sems
            dlast = [None] * self.n_dma_sems
            for o in self.ops[e]:
                if o.dma and o.inc != 16:
                    o.sem = stack.enter_context(nc.semaphore(f"cc_{e}_{o.idx}"))
                    o.val = o.inc
                elif o.dma:
                    k = dcnt % self.n_dma_sems
                    dcnt += 1
                    duse[k] += 1
                    o.sem = dma_sems[e][k]
                    o.val = 16 * duse[k]
                    o.guard = dlast[k]
                    dlast[k] = o
                elif o.signal:
                    o.sem = eng_sems[e][cnt // SEM_ROT]
                    o.val = cnt % SEM_ROT + 1
                    cnt += 1
        block = stack.enter_context(nc.Block())
        regs = {"pe": block.tensor, "act": block.scalar, "dve": block.vector,
                "pool": block.gpsimd, "sp": block.sync}

        def make_body(e):
            ops = self.ops[e]

            def body(eng):
                seen = {}
                for o in ops:
                    deps = list(o.deps)
                    if o.guard is not None:
                        deps.append(o.guard)
                    need = {}
                    for d in deps:
                        k = id(d.sem)
                        if seen.get(k, 0) >= d.val:
                            continue
                        if k not in need or need[k][1] < d.val:
                            need[k] = (d.sem, d.val)
                    for k, (sem, val) in need.items():
                        eng.wait_ge(sem, val)
                        seen[k] = val
                    if o.fn is None:
                        continue
                    ins = o.fn(eng)
                    if o.dma:
                        ins.then_inc(o.sem, o.inc)
                    elif o.signal:
                        ins.then_inc(o.sem, 1)
            return body

        for e in ENGS:
            regs[e](make_body(e))


D = 1024
DC = 8
GH = 4
GDK = 128
GDV = 256
GRANK = 16
AH = 8
HD = 64
EPS = 1e-6
GRID_W = 64
ROPE_BASE = 10000.0
N_MOD = 6


class Cfg:
    def __init__(self, NC=8, T=2048, CTX=256, DFF=2816, NG=4):
        self.NC, self.T, self.CTX, self.DFF, self.NG = NC, T, CTX, DFF, NG
        self.NT = T // 128
        self.NCT = CTX // 128
        self.FC = DFF // 128
        assert T % 256 == 0 and CTX % 128 == 0 and DFF % 128 == 0


def _sz(dt):
    return 2 if dt == BF16 else 4


class TT:
    __slots__ = ("ap", "b")

    def __init__(self, ap, b):
        self.ap = ap
        self.b = b


class KB:
    def __init__(self, cfg, phases=("A", "B", "C", "D", "E"), fused=True, debug=()):
        self.cfg = cfg
        self.phases = phases
        self.fused = fused
        self.debug = set(debug)
        nc = self.nc = bass.Bass("TRN2", target_bir_lowering=False)
        self.S = Sched(nc)
        self.st = ExitStack()
        self.AW = 52000
        self.arena = self.st.enter_context(nc.sbuf_tensor("arena", [128, self.AW], F32))
        self.aoff = 0
        self.psum = self.st.enter_context(nc.psum_tensor("ps", [128, 4096], F32))
        self.pb = [TT(self.psum[:, i * 512:(i + 1) * 512], self.S.buf(f"ps{i}")) for i in range(8)]
        self.in_names = []
        self.out_names = []
        self.dram_tt = {}

    def inp(self, name, shape, dt=F32):
        self.in_names.append(name)
        ap = self.nc.dram_tensor(name, list(shape), dt, kind="ExternalInput").ap()
        return TT(ap, self.S.buf(name))

    def dram(self, name, shape, dt, imports=(), exports=()):
        if name in imports:
            self.in_names.append(name)
            t = self.nc.dram_tensor(name, list(shape), dt, kind="ExternalInput")
        elif name in exports or name in self.debug:
            self.out_names.append(name)
            t = self.nc.dram_tensor(name, list(shape), dt, kind="ExternalOutput")
        else:
            t = self.nc.dram_tensor(name, list(shape), dt)
        tt = TT(t.ap(), self.S.buf(name))
        self.dram_tt[name] = (t, tt)
        return tt

    def sb(self, shape, dt, name=""):
        n = 1
        for s in shape[1:]:
            n *= s
        nw = (n * _sz(dt) + 3) // 4
        nw = (nw + 7) // 8 * 8
        assert self.aoff + nw <= self.AW, f"SBUF arena overflow at {name}: {self.aoff}+{nw}"
        v = self.arena[:, self.aoff:self.aoff + nw]
        self.aoff += nw
        if dt != F32:
            v = v.bitcast(dt)
        v = v[:, 0:n]
        if len(shape) == 3:
            v = v.rearrange("p (a b) -> p a b", a=shape[1])
        elif len(shape) == 4:
            v = v.rearrange("p (a b c) -> p a b c", a=shape[1], b=shape[2])
        if shape[0] != 128:
            v = v[0:shape[0]]
        return TT(v, self.S.buf(name))

    def mm(self, out, lhsT, rhs, start=True, stop=True, r=(), w=(), skip=False):
        self.S.op("pe", lambda e: e.matmul(out, lhsT, rhs, start=start, stop=stop,
                                           skip_group_check=skip), reads=r, writes=w)

    def tr(self, out, in_, ident, r=(), w=()):
        self.S.op("pe", lambda e: e.transpose(out, in_, ident), reads=r, writes=w)

    def act(self, out, in_, func, r=(), w=(), scale=1.0, bias=None, accum=None):
        def fn(e):
            kw = {}
            if bias is not None:
                kw["bias"] = bias
            if accum is not None:
                kw["accum_out"] = accum
            return e.activation(out=out, in_=in_, func=func, scale=scale, **kw)
        self.S.op("act", fn, reads=r, writes=w)

    def tt(self, eng, out, a, b, op, r=(), w=()):
        self.S.op(eng, lambda e: e.tensor_tensor(out=out, in0=a, in1=b, op=op), reads=r, writes=w)

    def ts(self, eng, out, a, s1, op0, s2=None, op1=None, r=(), w=()):
        def fn(e):
            if op1 is None:
                return e.tensor_scalar(out=out, in0=a, scalar1=s1, scalar2=None, op0=op0)
            return e.tensor_scalar(out=out, in0=a, scalar1=s1, scalar2=s2, op0=op0, op1=op1)
        self.S.op(eng, fn, reads=r, writes=w)

    def stt(self, out, in0, scalar, in1, op0, op1, r=(), w=()):
        self.S.op("dve", lambda e: e.scalar_tensor_tensor(out=out, in0=in0, scalar=scalar, in1=in1,
                                                          op0=op0, op1=op1), reads=r, writes=w)

    def cp(self, eng, out, in_, r=(), w=()):
        if eng == "act":
            self.S.op("act", lambda e: e.copy(out=out, in_=in_), reads=r, writes=w)
        else:
            self.S.op(eng, lambda e: e.tensor_copy(out=out, in_=in_), reads=r, writes=w)

    def recip(self, out, in_, r=(), w=()):
        self.S.op("dve", lambda e: e.reciprocal(out=out, in_=in_), reads=r, writes=w)

    def memset(self, eng, ap, val, w=()):
        self.S.op(eng, lambda e: e.memset(ap, val), writes=w)

    def dma(self, q, out, in_, r=(), w=(), slow=False):
        if slow:
            self.S.op(q, lambda e: e.dma_start(out=out, in_=in_, allow_slow_non_contiguous=True),
                      reads=r, writes=w, dma=True)
        else:
            self.S.op(q, lambda e: e.dma_start(out=out, in_=in_), reads=r, writes=w, dma=True)

    def declare_inputs(self):
        c = self.cfg
        I = {}
        I["x"] = self.inp("x", [c.T, D])
        I["ctx"] = self.inp("ctx", [c.CTX, D])
        I["c"] = self.inp("c", [1, D])
        I["c_ctx"] = self.inp("c_ctx", [1, D])
        I["mod_w"] = self.inp("mod_w", [2, D, 6 * D])
        I["mod_b"] = self.inp("mod_b", [2, 6 * D])
        I["norm_mix"] = self.inp("norm_mix", [2, D])
        I["norm_ffn"] = self.inp("norm_ffn", [2, D])
        I["gla_wq"] = self.inp("gla_wq", [1, D, 512])
        I["gla_wk"] = self.inp("gla_wk", [1, D, 512])
        I["gla_wv"] = self.inp("gla_wv", [1, D, D])
        I["gla_wr"] = self.inp("gla_wr", [1, D, D])
        I["gla_wg1"] = self.inp("gla_wg1", [1, 2, D, 16])
        I["gla_wg2"] = self.inp("gla_wg2", [1, 2, 16, 512])
        I["gla_bg"] = self.inp("gla_bg", [1, 2, 512])
        I["gla_norm"] = self.inp("gla_norm", [1, 256])
        I["gla_wo"] = self.inp("gla_wo", [1, D, D])
        I["diff_wq"] = self.inp("diff_wq", [1, D, D])
        I["diff_wk"] = self.inp("diff_wk", [1, D, D])
        I["diff_wv"] = self.inp("diff_wv", [1, D, D])
        for n in ("diff_lq1", "diff_lk1", "diff_lq2", "diff_lk2"):
            I[n] = self.inp(n, [1, 64])
        I["diff_subln"] = self.inp("diff_subln", [1, 128])
        I["diff_wo"] = self.inp("diff_wo", [1, D, D])
        I["ffn_wup"] = self.inp("ffn_wup", [2, D, 2 * c.DFF])
        I["ffn_conv"] = self.inp("ffn_conv", [2, 3, 2 * c.DFF])
        I["ffn_conv_b"] = self.inp("ffn_conv_b", [2, 2 * c.DFF])
        I["ffn_wdown"] = self.inp("ffn_wdown", [2, c.DFF, D])
        I["final_norm"] = self.inp("final_norm", [1, D])
        I["k_ident"] = self.inp("k_ident", [128, 128])
        I["k_ltf"] = self.inp("k_ltf", [128, 128])
        I["k_ltb"] = self.inp("k_ltb", [128, 128])
        I["k_mf"] = self.inp("k_mf", [128, 512])
        I["k_mb"] = self.inp("k_mb", [128, 512])
        I["k_rm"] = self.inp("k_rm", [128, 128])
        I["k_cos"] = self.inp("k_cos", [128, c.T])
        I["k_sin"] = self.inp("k_sin", [128, c.T])
        I["k_cm"] = self.inp("k_cm", [128, 6 * c.NC])
        I["k_n16"] = self.inp("k_n16", [128, 2])
        self.I = I

    def vec_fm(self, src_ap, n, name):
        stage = self.stage
        bank = self.pb[7]
        self.dma("sp", stage.ap[0:n, :], src_ap, w=[stage.b])
        self.tr(bank.ap[:, 0:n], stage.ap[0:n, :], self.ident.ap[0:n, 0:n],
                r=[stage.b, self.ident.b], w=[bank.b])
        out = self.sb([128, n], F32, name)
        self.cp("dve", out.ap, bank.ap[:, 0:n], r=[bank.b], w=[out.b])
        return out

    def setup(self):
        c = self.cfg
        I = self.I
        self.ident = self.sb([128, 128], F32, "ident")
        self.dma("sp", self.ident.ap, I["k_ident"].ap, w=[self.ident.b])
        self.identb = self.sb([128, 128], BF16, "identb")
        self.cp("dve", self.identb.ap, self.ident.ap, r=[self.ident.b], w=[self.identb.b])
        self.onesb = self.sb([128, 128], BF16, "onesb")
        self.memset("dve", self.onesb.ap, 1.0, w=[self.onesb.b])
        self.onesf = self.sb([128, 128], F32, "onesf")
        self.memset("dve", self.onesf.ap, 1.0, w=[self.onesf.b])
        self.cm = self.sb([128, 6 * c.NC], F32, "cm")
        self.dma("sp", self.cm.ap, I["k_cm"].ap, w=[self.cm.b])
        self.n16 = self.sb([128, 2], F32, "n16")
        self.dma("sp", self.n16.ap, I["k_n16"].ap, w=[self.n16.b])
        self.stage = self.sb([128, 128], F32, "stage")
        self.epsb = self.sb([128, 1], F32, "epsb")
        self.oneb = self.sb([128, 1], F32, "oneb")
        self.memset("dve", self.oneb.ap, 1.0, w=[self.oneb.b])
        self.memset("dve", self.epsb.ap, EPS, w=[self.epsb.b])

        def rows(ap1, n):
            return ap1.rearrange("o (n p) -> (o n) p", p=128)

        cf = self.vec_fm(rows(I["c"].ap, 8), 8, "cf")
        ccf = self.vec_fm(rows(I["c_ctx"].ap, 8), 8, "ccf")
        sc2 = self.sb([128, 8, 2], F32, "sc2")
        self.act(sc2.ap[:, :, 0], cf.ap, AF.Silu, r=[cf.b], w=[sc2.b])
        self.act(sc2.ap[:, :, 1], ccf.ap, AF.Silu, r=[ccf.b, sc2.b], w=[sc2.b])
        self.modx, self.modc = [], []
        mark = self.aoff
        mw = [self.sb([128, 8, 512], F32, f"mw{i}") for i in range(2)]
        k = 0
        mods = []
        for l in range(2):
            bank = self.pb[l]
            for g in range(12):
                w = mw[k % 2]
                k += 1
                self.dma("sp", w.ap, I["mod_w"].ap[l].rearrange("(kc p) n -> p kc n", p=128)
                         [:, :, g * 512:(g + 1) * 512], w=[w.b])
                for j in range(4):
                    col = g * 4 + j
                    for kc in range(8):
                        self.mm(bank.ap[:, col * 2:col * 2 + 2], w.ap[:, kc, j * 128:(j + 1) * 128],
                                sc2.ap[:, kc, :], start=(kc == 0), stop=(kc == 7),
                                r=[w.b, sc2.b], w=[bank.b])
            mods.append(bank)
        self.free_to(mark)
        for l in range(2):
            mb = self.vec_fm(I["mod_b"].ap[l:l + 1, :].rearrange("o (n p) -> (o n) p", p=128), 48, f"mb{l}")
            mx = self.sb([128, 48], F32, f"modx{l}")
            mc = self.sb([128, 48], F32, f"modc{l}")
            pv = mods[l].ap[:, 0:96].rearrange("p (n t) -> p n t", t=2)
            self.tt("dve", mx.ap, pv[:, :, 0], mb.ap, ALU.add, r=[mods[l].b, mb.b], w=[mx.b])
            self.tt("dve", mc.ap, pv[:, :, 1], mb.ap, ALU.add, r=[mods[l].b, mb.b], w=[mc.b])
            self.modx.append(mx)
            self.modc.append(mc)
        self.nA = {}
        for l in range(2):
            for which, key, base in (("mix", "norm_mix", 0), ("ffn", "norm_ffn", 3)):
                g = self.vec_fm(rows(I[key].ap[l:l + 1, :], 8), 8, f"g_{which}{l}")
                for sname, mod in (("x", self.modx[l]), ("c", self.modc[l])):
                    A = self.sb([128, 8], F32, f"A_{which}{l}{sname}")
                    self.stt(A.ap, mod.ap[:, (base + 1) * 8:(base + 2) * 8], 1.0, g.ap, ALU.add, ALU.mult,
                             r=[mod.b, g.b], w=[A.b])
                    self.nA[(which, l, sname)] = (A, mod, base)
        self.gfin = self.vec_fm(rows(I["final_norm"].ap, 8), 8, "gfin")
        self.cw = []
        for l in range(2):
            lst = []
            for kk in range(3):
                lst.append(self.vec_fm(I["ffn_conv"].ap[l, kk:kk + 1, :].rearrange("o (n p) -> (o n) p", p=128),
                                       2 * c.FC, f"cw{l}{kk}"))
            lst.append(self.vec_fm(I["ffn_conv_b"].ap[l:l + 1, :].rearrange("o (n p) -> (o n) p", p=128),
                                   2 * c.FC, f"cb{l}"))
            self.cw.append(lst)
        self.gn = self.vec_fm(rows(I["gla_norm"].ap, 2), 2, "gn")
        self.ltf = self.sb([128, 128], F32, "ltf")
        self.ltb = self.sb([128, 128], F32, "ltb")
        self.dma("sp", self.ltf.ap, I["k_ltf"].ap, w=[self.ltf.b])
        self.dma("sp", self.ltb.ap, I["k_ltb"].ap, w=[self.ltb.b])
        self.persist_mark = self.aoff
        self.S.barrier()

    def scal(self, which, l, sname):
        A, mod, base = self.nA[(which, l, sname)]
        return A, mod.ap[:, base * 8:(base + 1) * 8], mod.ap[:, (base + 2) * 8:(base + 3) * 8], mod

    def phase_reset(self):
        self.S.barrier()
        self.aoff = self.persist_mark
        self._mn = None

    def free_to(self, mark):
        self.S.barrier()
        self.aoff = mark
        self._mn = None

    def modnorm(self, src, col0, ntok, A, shift, rdeps, dst, dcol0, gain_only=None, out_f32=None):
        mark = self.aoff
        if getattr(self, "_mn", None) is None:
            xb = [self.sb([128, 8, 512], F32, f"mn_x{i}") for i in range(2)]
            sq = [self.sb([128, 8, 512], BF16, f"mn_sq{i}") for i in range(2)]
            rt = [self.sb([128, 512], F32, f"mn_rt{i}") for i in range(2)]
            rs = [self.sb([128, 512], F32, f"mn_rs{i}") for i in range(2)]
            tm = [self.sb([128, 512], F32, f"mn_t{i}") for i in range(3)]
            self._mn = (mark, xb, sq, rt, rs, tm)
        mark, xb, sq, rt, rs, tm = self._mn
        sv = src.ap.rearrange("(c p) t -> p c t", p=128)
        it = 0
        k3 = 0
        for t0 in range(0, ntok, 512):
            n = min(512, ntok - t0)
            x, q, r_, s_ = xb[it % 2], sq[it % 2], rt[it % 2], rs[it % 2]
            bank = self.pb[it % 2]
            it += 1
            self.dma("sp", x.ap[:, :, 0:n], sv[:, :, col0 + t0:col0 + t0 + n], r=[src.b], w=[x.b], slow=(n < 8))
            self.act(q.ap[:, :, 0:n], x.ap[:, :, 0:n], AF.Square, r=[x.b], w=[q.b])
            for cc in range(8):
                self.mm(bank.ap[:, 0:n], self.onesb.ap, q.ap[:, cc, 0:n], start=(cc == 0), stop=(cc == 7),
                        r=[q.b, self.onesb.b], w=[bank.b])
            self.act(r_.ap[:, 0:n], bank.ap[:, 0:n], AF.Sqrt, r=[bank.b], w=[r_.b], scale=1.0 / D, bias=self.epsb.ap)
            self.recip(s_.ap[:, 0:n], r_.ap[:, 0:n], r=[r_.b], w=[s_.b])
            for cc in range(8):
                if out_f32 is not None:
                    self.stt(out_f32.ap[:, cc, t0:t0 + n], x.ap[:, cc, 0:n], A.ap[:, cc:cc + 1], s_.ap[:, 0:n],
                             ALU.mult, ALU.mult, r=[x.b, s_.b, A.b] + rdeps, w=[out_f32.b])
                    continue
                t_ = tm[k3 % 3]
                k3 += 1
                self.stt(t_.ap[:, 0:n], x.ap[:, cc, 0:n], A.ap[:, cc:cc + 1], s_.ap[:, 0:n],
                         ALU.mult, ALU.mult, r=[x.b, s_.b, A.b] + rdeps, w=[t_.b])
                self.act(dst.ap[:, cc, dcol0 + t0:dcol0 + t0 + n], t_.ap[:, 0:n], AF.Identity,
                         r=[t_.b] + rdeps, w=[dst.b], bias=shift[:, cc:cc + 1])
        return mark

    def declare_scratch(self, imports=(), exports=()):
        c = self.cfg
        d = lambda n, s, t: self.dram(n, s, t, imports, exports)
        self.xT_d = d("xT_d", [D, c.T], F32)
        self.cT_d = d("cT_d", [D, c.CTX], F32)
        self.summ_d = d("summ_d", [128, 2056], F32)
        self.summ_all = d("summ_all", [c.NC * 128, 2056], F32)
        self.cst_d = d("cst_d", [128, 2048], F32)
        self.sbprev_d = d("sbprev_d", [(c.NT + c.NCT) * 128, 1024], BF16)
        self.halo_d = [d(f"halo{l}_d", [128, 16], F32) for l in range(2)]
        self.halo_all = [d(f"halo{l}_all", [c.NC * 128, 16], F32) for l in range(2)]
        self.qT_d = d("qT_d", [D, c.T], BF16)
        self.kv_loc = d("kv_loc", [2048, c.T], BF16)
        self.kvc_d = d("kvc_d", [2048, c.CTX], BF16)
        self.kv_all = d("kv_all", [c.NC * 2048, c.T], BF16)

    def exchange(self, loc, gat):
        if not self.fused:
            return
        tl = self.dram_tt[[k for k, v in self.dram_tt.items() if v[1] is loc][0]][0]
        tg = self.dram_tt[[k for k, v in self.dram_tt.items() if v[1] is gat][0]][0]
        rg = [list(range(self.cfg.NC))]
        self.S.op("pool", lambda e: e.collective_compute("AllGather", ALU.bypass, replica_groups=rg,
                                                         ins=[tl.ap().opt()], outs=[tg.ap().opt()]),
                  reads=[loc.b], writes=[gat.b], dma=True, inc=1)

    def prep_transpose(self, src, dst, ntiles):
        xin = [self.sb([128, D], F32, f"pt_in{i}") for i in range(2)]
        xo = [self.sb([128, 4, 128], F32, f"pt_o{i}") for i in range(4)]
        dv = dst.ap.rearrange("(c p) t -> p c t", p=128)
        k = 0
        for i in range(ntiles):
            xi = xin[i % 2]
            self.dma("sp", xi.ap, src.ap[i * 128:(i + 1) * 128, :], r=[src.b], w=[xi.b])
            for half in range(2):
                bank = self.pb[k % 4]
                o = xo[k % 4]
                k += 1
                for cc in range(4):
                    cidx = half * 4 + cc
                    self.tr(bank.ap[:, cc * 128:(cc + 1) * 128], xi.ap[:, cidx * 128:(cidx + 1) * 128],
                            self.ident.ap, r=[xi.b, self.ident.b], w=[bank.b])
                eng = "act" if half == 0 else "dve"
                self.cp(eng, o.ap.rearrange("p a b -> p (a b)"), bank.ap, r=[bank.b], w=[o.b])
                self.dma("sp", dv[:, half * 4:(half + 1) * 4, i * 128:(i + 1) * 128], o.ap, r=[o.b], w=[dst.b])

    def gla_load(self):
        c = self.cfg
        I = self.I
        G = type("G", (), {})()
        self.G = G
        ntot = c.T + c.CTX
        G.hT = self.sb([128, 8, ntot], BF16, "g_hT")
        A, sh, _, mod = self.scal("mix", 0, "x")
        mark = self.modnorm(self.xT_d, 0, c.T, A, sh, [mod.b], G.hT, 0)
        A, sh, _, mod = self.scal("mix", 0, "c")
        self.modnorm(self.cT_d, 0, c.CTX, A, sh, [mod.b], G.hT, c.T)
        self.free_to(mark)

        def wload(name, src3, ncol):
            w = self.sb([128, 8, ncol], BF16, name)
            self.dma("pool", w.ap, src3.rearrange("(c p) n -> p c n", p=128), w=[w.b])
            return w
        G.wq = wload("g_wq", I["gla_wq"].ap[0], 512)
        G.wk = wload("g_wk", I["gla_wk"].ap[0], 512)
        G.wv = wload("g_wv", I["gla_wv"].ap[0], 1024)
        G.wr = wload("g_wr", I["gla_wr"].ap[0], 1024)
        G.wo = wload("g_wo", I["gla_wo"].ap[0], 1024)
        G.wg1 = self.sb([128, 8, 32], BF16, "g_wg1")
        for d_ in range(2):
            self.dma("pool", G.wg1.ap[:, :, d_ * 16:(d_ + 1) * 16],
                     I["gla_wg1"].ap[0, d_].rearrange("(c p) n -> p c n", p=128), w=[G.wg1.b], slow=True)
        G.wg2 = []
        G.z1 = []
        for d_ in range(2):
            w2 = self.sb([32, 512], BF16, f"g_wg2{d_}")
            self.dma("pool", w2.ap[0:16, :], I["gla_wg2"].ap[0, d_], w=[w2.b])
            self.dma("pool", w2.ap[16:17, :], I["gla_bg"].ap[0, d_:d_ + 1, :], r=[w2.b], w=[w2.b])
            G.wg2.append(w2)
            z1 = self.sb([32, 128], BF16, f"g_z1{d_}")
            self.memset("dve", z1.ap, 1.0, w=[z1.b])
            G.z1.append(z1)
        G.mask = []
        for d_, key in enumerate(("k_mf", "k_mb")):
            m = self.sb([128, 512], F32, f"g_mask{d_}")
            self.dma("sp", m.ap, I[key].ap, w=[m.b])
            G.mask.append(m)
        G.lt = [self.ltf, self.ltb]
        G.v_sb = self.sb([128, 1024], BF16, "g_v")
        G.e_t = self.sb([128, 512], F32, "g_e")
        G.sp_t = self.sb([128, 512], F32, "g_sp")
        G.enk = self.sb([128, 512], F32, "g_enk")
        G.eqT = self.sb([128, 512], F32, "g_eqT")
        G.ekT = self.sb([128, 512], F32, "g_ekT")
        G.kt = self.sb([128, 512], BF16, "g_kt")
        G.kT = self.sb([128, 512], BF16, "g_kT")
        G.qT = [self.sb([128, 512], BF16, f"g_qT{d_}") for d_ in range(2)]
        G.sc = [self.sb([128, 512], BF16, f"g_sc{d_}") for d_ in range(2)]
        G.a = [self.sb([128, 4], F32, f"g_a{d_}") for d_ in range(2)]
        G.sq = self.sb([128, 1024], BF16, "g_sq")
        G.rt = self.sb([128, 512], F32, "g_rt")
        G.rstd = self.sb([128, 512], F32, "g_rstd")
        G.rT = self.sb([128, 1024], F32, "g_rT")
        G.tmp = self.sb([128, 1024], F32, "g_tmp")
        G.onT = self.sb([128, 1024], BF16, "g_onT")
        G.xt = self.sb([128, 8, 128], F32, "g_xt")
        G.Sf = self.sb([128, 1024], F32, "g_Sf")
        G.Sfb = self.sb([128, 1024], BF16, "g_Sfb")
        G.Sb = self.sb([128, 1024], F32, "g_Sb")
        G.Sbb = self.sb([128, 1024], BF16, "g_Sbb")
        G.Bb = self.sb([128, 1024], F32, "g_Bb")
        G.pf = self.sb([128, 4], F32, "g_pf")
        G.pbk = self.sb([128, 4], F32, "g_pbk")
        G.sbp = [self.sb([128, 1024], BF16, f"g_sbp{i}") for i in range(2)]
        G.cst = self.sb([128, 2048], F32, "g_cst")

    def gla_tile(self, col, mode, sidx=None, xdst=None, xcol=None, gate=None, gdep=None):
        G = self.G
        P = self.pb
        hT = G.hT.ap[:, :, col:col + 128]
        hb = [G.hT.b]
        full = mode == "p2b"
        dirs = {"p1": (0, 1), "p2a": (1,), "p2b": (0, 1)}[mode]
        B0, B1, B2, B3, B4, B5, B6, B7 = P
        if full:
            for h in range(4):
                for cc in range(8):
                    self.mm(B0.ap[:, h * 128:(h + 1) * 128], G.wq.ap[:, cc, h * 128:(h + 1) * 128], hT[:, cc, :],
                            start=(cc == 0), stop=(cc == 7), r=hb + [G.wq.b], w=[B0.b])
            for h in range(4):
                for cc in range(8):
                    self.mm(B1.ap[:, h * 128:(h + 1) * 128], G.wk.ap[:, cc, h * 128:(h + 1) * 128], hT[:, cc, :],
                            start=(cc == 0), stop=(cc == 7), r=hb + [G.wk.b], w=[B1.b])
        for cc in range(8):
            self.mm(B2.ap, hT[:, cc, :], G.wk.ap[:, cc, :], start=(cc == 0), stop=(cc == 7),
                    r=hb + [G.wk.b], w=[B2.b])
        for half, Bv in enumerate((B3, B4)):
            for cc in range(8):
                self.mm(Bv.ap, hT[:, cc, :], G.wv.ap[:, cc, half * 512:(half + 1) * 512],
                        start=(cc == 0), stop=(cc == 7), r=hb + [G.wv.b], w=[Bv.b])
        self.cp("act", G.v_sb.ap[:, 0:512], B3.ap, r=[B3.b], w=[G.v_sb.b])
        self.cp("act", G.v_sb.ap[:, 512:1024], B4.ap, r=[B4.b, G.v_sb.b], w=[G.v_sb.b])
        if full:
            self.cp("pool", G.Sfb.ap, G.Sf.ap, r=[G.Sf.b], w=[G.Sfb.b])
            sbp = G.sbp[sidx % 2]
            self.dma("sp", sbp.ap, self.sbprev_d.ap[sidx * 128:(sidx + 1) * 128, :], r=[self.sbprev_d.b], w=[sbp.b])
        for d_ in dirs:
            z1 = G.z1[d_]
            for cc in range(8):
                self.mm(B5.ap[0:16, 0:128], G.wg1.ap[:, cc, d_ * 16:(d_ + 1) * 16], hT[:, cc, :],
                        start=(cc == 0), stop=(cc == 7), r=hb + [G.wg1.b], w=[B5.b])
            self.cp("dve", z1.ap[0:16, :], B5.ap[0:16, 0:128], r=[B5.b], w=[z1.b])
            self.mm(B6.ap, z1.ap[0:17, :], G.wg2[d_].ap[0:17, :], r=[z1.b, G.wg2[d_].b], w=[B6.b])
            self.act(G.e_t.ap, B6.ap, AF.Exp, r=[B6.b], w=[G.e_t.b], scale=-1.0)
            self.act(G.sp_t.ap, G.e_t.ap, AF.Ln, r=[G.e_t.b], w=[G.sp_t.b], bias=self.oneb.ap)
            need_M = not (full and d_ == 1)
            if need_M:
                self.mm(B5.ap, G.lt[d_].ap, G.sp_t.ap, r=[G.lt[d_].b, G.sp_t.b], w=[B5.b])
                self.act(G.enk.ap, B5.ap, AF.Exp, r=[B5.b], w=[G.enk.b], scale=-1.0)
                self.tt("dve", G.kt.ap, B2.ap, G.enk.ap, ALU.mult, r=[B2.b, G.enk.b], w=[G.kt.b])
            if full:
                for h in range(4):
                    self.mm(B6.ap[:, h * 128:(h + 1) * 128], G.sp_t.ap[:, h * 128:(h + 1) * 128], G.lt[d_].ap,
                            r=[G.sp_t.b, G.lt[d_].b], w=[B6.b])
                self.act(G.eqT.ap, B6.ap, AF.Exp, r=[B6.b], w=[G.eqT.b])
                self.act(G.ekT.ap, B6.ap, AF.Exp, r=[B6.b], w=[G.ekT.b], scale=-1.0)
                ecol = 127 if d_ == 0 else 0
                self.cp("dve", G.a[d_].ap, G.eqT.ap.rearrange("p (h t) -> p h t", h=4)[:, :, ecol],
                        r=[G.eqT.b], w=[G.a[d_].b])
                self.tt("dve", G.qT[d_].ap, B0.ap, G.eqT.ap, ALU.mult, r=[B0.b, G.eqT.b], w=[G.qT[d_].b])
                self.tt("dve", G.kT.ap, B1.ap, G.ekT.ap, ALU.mult, r=[B1.b, G.ekT.b], w=[G.kT.b])
                for h in range(4):
                    self.mm(B7.ap[:, h * 128:(h + 1) * 128], G.kT.ap[:, h * 128:(h + 1) * 128],
                            G.qT[d_].ap[:, h * 128:(h + 1) * 128], r=[G.kT.b, G.qT[d_].b], w=[B7.b])
                self.tt("dve", G.sc[d_].ap, B7.ap, G.mask[d_].ap, ALU.mult, r=[B7.b, G.mask[d_].b], w=[G.sc[d_].b])
            else:
                for h in range(4):
                    self.mm(B6.ap[:, 2 * h:2 * h + 2], G.sp_t.ap[:, h * 128:(h + 1) * 128], self.n16.ap,
                            r=[G.sp_t.b, self.n16.b], w=[B6.b])
                self.act(G.a[d_].ap, B6.ap[:, 0:8].rearrange("p (h t) -> p h t", t=2)[:, :, 0], AF.Exp,
                         r=[B6.b], w=[G.a[d_].b])
            if not need_M:
                continue
            for h in range(4):
                Bm = B5 if h < 2 else B6
                self.mm(Bm.ap[:, (h % 2) * 256:(h % 2) * 256 + 256], G.kt.ap[:, h * 128:(h + 1) * 128],
                        G.v_sb.ap[:, h * 256:(h + 1) * 256], r=[G.kt.b, G.v_sb.b], w=[Bm.b])
            a = G.a[d_]

            def mslice(h):
                Bm = B5 if h < 2 else B6
                return Bm, Bm.ap[:, (h % 2) * 256:(h % 2) * 256 + 256]
            if d_ == 0 and mode in ("p1", "p2b"):
                for h in range(4):
                    Bm, ms = mslice(h)
                    sl = G.Sf.ap[:, h * 256:(h + 1) * 256]
                    self.ts("dve", sl, sl, a.ap[:, h:h + 1], ALU.mult, r=[G.Sf.b, a.b], w=[G.Sf.b])
                    self.stt(sl, ms, a.ap[:, h:h + 1], sl, ALU.mult, ALU.add, r=[Bm.b, a.b, G.Sf.b], w=[G.Sf.b])
                if mode == "p1":
                    self.tt("dve", G.pf.ap, G.pf.ap, a.ap, ALU.mult, r=[G.pf.b, a.b], w=[G.pf.b])
            elif mode == "p1" and d_ == 1:
                self.tt("dve", G.pbk.ap, G.pbk.ap, a.ap, ALU.mult, r=[G.pbk.b, a.b], w=[G.pbk.b])
                for h in range(4):
                    Bm, ms = mslice(h)
                    sl = G.Bb.ap[:, h * 256:(h + 1) * 256]
                    self.stt(sl, ms, G.pbk.ap[:, h:h + 1], sl, ALU.mult, ALU.add,
                             r=[Bm.b, G.pbk.b, G.Bb.b], w=[G.Bb.b])
            elif mode == "p2a":
                self.cp("pool", G.Sbb.ap, G.Sb.ap, r=[G.Sb.b], w=[G.Sbb.b])
                self.dma("sp", self.sbprev_d.ap[sidx * 128:(sidx + 1) * 128, :], G.Sbb.ap,
                         r=[G.Sbb.b], w=[self.sbprev_d.b])
                for h in range(4):
                    Bm, ms = mslice(h)
                    sl = G.Sb.ap[:, h * 256:(h + 1) * 256]
                    self.ts("dve", sl, sl, a.ap[:, h:h + 1], ALU.mult, r=[G.Sb.b, a.b], w=[G.Sb.b])
                    self.stt(sl, ms, a.ap[:, h:h + 1], sl, ALU.mult, ALU.add, r=[Bm.b, a.b, G.Sb.b], w=[G.Sb.b])
        if full:
            self.gla_out_and_readout(col, sidx, xdst, xcol, gate, gdep)

    def gla_out_and_readout(self, col, sidx, xdst, xcol, gate, gdep):
        G = self.G
        B0, B1, B2, B3, B4, B5, B6, B7 = self.pb
        hT = G.hT.ap[:, :, col:col + 128]
        hb = [G.hT.b]
        sbp = G.sbp[sidx % 2]
        for hv in range(8):
            h, vc = hv // 2, hv % 2
            Bo = B3 if hv < 4 else B4
            o = Bo.ap[:, (hv % 4) * 128:(hv % 4) * 128 + 128]
            vs = slice(h * 256 + vc * 128, h * 256 + vc * 128 + 128)
            hs = slice(h * 128, h * 128 + 128)
            self.mm(o, G.v_sb.ap[:, vs], G.sc[0].ap[:, hs], start=True, stop=False,
                    r=[G.v_sb.b, G.sc[0].b], w=[Bo.b])
            self.mm(o, G.Sfb.ap[:, vs], G.qT[0].ap[:, hs], start=False, stop=False,
                    r=[G.Sfb.b, G.qT[0].b], w=[Bo.b])
            self.mm(o, G.v_sb.ap[:, vs], G.sc[1].ap[:, hs], start=False, stop=False,
                    r=[G.sc[1].b], w=[Bo.b])
            self.mm(o, sbp.ap[:, vs], G.qT[1].ap[:, hs], start=False, stop=True,
                    r=[sbp.b, G.qT[1].b], w=[Bo.b])
        self.act(G.sq.ap[:, 0:512], B3.ap, AF.Square, r=[B3.b], w=[G.sq.b])
        self.act(G.sq.ap[:, 512:1024], B4.ap, AF.Square, r=[B4.b, G.sq.b], w=[G.sq.b])
        for h in range(4):
            for vc in range(2):
                hv = 2 * h + vc
                self.mm(B0.ap[:, h * 128:(h + 1) * 128], self.onesb.ap, G.sq.ap[:, hv * 128:(hv + 1) * 128],
                        start=(vc == 0), stop=(vc == 1), r=[G.sq.b, self.onesb.b], w=[B0.b])
        self.act(G.rt.ap, B0.ap, AF.Sqrt, r=[B0.b], w=[G.rt.b], scale=1.0 / GDV, bias=self.epsb.ap)
        self.recip(G.rstd.ap, G.rt.ap, r=[G.rt.b], w=[G.rstd.b])
        for cc in range(8):
            Br = B1 if cc < 4 else B2
            for kc in range(8):
                self.mm(Br.ap[:, (cc % 4) * 128:(cc % 4) * 128 + 128], G.wr.ap[:, kc, cc * 128:(cc + 1) * 128],
                        hT[:, kc, :], start=(kc == 0), stop=(kc == 7), r=hb + [G.wr.b], w=[Br.b])
        self.act(G.rT.ap[:, 0:512], B1.ap, AF.Silu, r=[B1.b], w=[G.rT.b])
        self.act(G.rT.ap[:, 512:1024], B2.ap, AF.Silu, r=[B2.b, G.rT.b], w=[G.rT.b])
        tv = G.tmp.ap.rearrange("p (h v t) -> p h v t", h=4, v=2)
        rv = G.rT.ap.rearrange("p (h v t) -> p h v t", h=4, v=2)
        ov = G.onT.ap.rearrange("p (h v t) -> p h v t", h=4, v=2)
        rsv = G.rstd.ap.rearrange("p (h t) -> p h t", h=4)
        for bi, Bo in enumerate((B3, B4)):
            bv = Bo.ap.rearrange("p (h v t) -> p h v t", h=2, v=2)
            for vc in range(2):
                self.tt("dve", tv[:, 2 * bi:2 * bi + 2, vc, :], bv[:, :, vc, :], rsv[:, 2 * bi:2 * bi + 2, :],
                        ALU.mult, r=[Bo.b, G.rstd.b], w=[G.tmp.b])
        for vc in range(2):
            self.stt(ov[:, :, vc, :], tv[:, :, vc, :], self.gn.ap[:, vc:vc + 1], rv[:, :, vc, :],
                     ALU.mult, ALU.mult, r=[G.tmp.b, G.rT.b, self.gn.b], w=[G.onT.b])
        for dc in range(8):
            Bd = B5 if dc < 4 else B6
            for hv in range(8):
                self.mm(Bd.ap[:, (dc % 4) * 128:(dc % 4) * 128 + 128], G.wo.ap[:, hv, dc * 128:(dc + 1) * 128],
                        G.onT.ap[:, hv * 128:(hv + 1) * 128], start=(hv == 0), stop=(hv == 7),
                        r=[G.onT.b, G.wo.b], w=[Bd.b])
        xv = xdst.ap.rearrange("(c p) t -> p c t", p=128)[:, :, xcol:xcol + 128]
        self.dma("sp", G.xt.ap, xv, r=[xdst.b], w=[G.xt.b])
        for dc in range(8):
            Bd = B5 if dc < 4 else B6
            self.stt(G.xt.ap[:, dc, :], Bd.ap[:, (dc % 4) * 128:(dc % 4) * 128 + 128], gate[:, dc:dc + 1],
                     G.xt.ap[:, dc, :], ALU.mult, ALU.add, r=[Bd.b, G.xt.b, gdep], w=[G.xt.b])
        self.dma("sp", xv, G.xt.ap, r=[G.xt.b], w=[xdst.b])

    def gla_state_zero(self):
        G = self.G
        for t_ in (G.Sf, G.Sb, G.Bb):
            self.memset("pool", t_.ap, 0.0, w=[t_.b])
        for t_ in (G.pf, G.pbk):
            self.memset("pool", t_.ap, 1.0, w=[t_.b])

    def phase_A(self):
        c = self.cfg
        G = None
        self.prep_transpose(self.I["x"], self.xT_d, c.NT)
        self.prep_transpose(self.I["ctx"], self.cT_d, c.NCT)
        self.phase_reset()
        self.gla_load()
        G = self.G
        _, _, gate_c, modc = self.scal("mix", 0, "c")
        self.gla_state_zero()
        for i in range(c.NCT):
            self.gla_tile(c.T + i * 128, "p1")
        self.cp("pool", G.cst.ap[:, 0:1024], G.Sf.ap, r=[G.Sf.b], w=[G.cst.b])
        self.cp("pool", G.cst.ap[:, 1024:2048], G.Bb.ap, r=[G.Bb.b, G.cst.b], w=[G.cst.b])
        self.dma("sp", self.cst_d.ap, G.cst.ap, r=[G.cst.b], w=[self.cst_d.b])
        self.gla_state_zero()
        for i in reversed(range(c.NCT)):
            self.gla_tile(c.T + i * 128, "p2a", sidx=c.NT + i)
        self.gla_state_zero()
        for i in range(c.NCT):
            self.gla_tile(c.T + i * 128, "p2b", sidx=c.NT + i, xdst=self.cT_d, xcol=i * 128,
                          gate=gate_c, gdep=modc.b)
        self.gla_state_zero()
        for i in range(c.NT):
            self.gla_tile(i * 128, "p1")
        summ = self.sb([128, 2056], F32, "summ")
        self.cp("pool", summ.ap[:, 0:1024], G.Sf.ap, r=[G.Sf.b], w=[summ.b])
        self.cp("pool", summ.ap[:, 1024:2048], G.Bb.ap, r=[G.Bb.b, summ.b], w=[summ.b])
        self.cp("pool", summ.ap[:, 2048:2052], G.pf.ap, r=[G.pf.b, summ.b], w=[summ.b])
        self.cp("pool", summ.ap[:, 2052:2056], G.pbk.ap, r=[G.pbk.b, summ.b], w=[summ.b])
        self.dma("sp", self.summ_d.ap, summ.ap, r=[summ.b], w=[self.summ_d.b])
        self.phase_reset()

    def phase_B(self):
        c = self.cfg
        self.gla_load()
        G = self.G
        _, _, gate_x, modx = self.scal("mix", 0, "x")
        self.dma("sp", G.cst.ap, self.cst_d.ap, r=[self.cst_d.b], w=[G.cst.b])
        self.cp("pool", G.Sf.ap, G.cst.ap[:, 0:1024], r=[G.cst.b], w=[G.Sf.b])
        self.cp("pool", G.Sb.ap, G.cst.ap[:, 1024:2048], r=[G.cst.b], w=[G.Sb.b])
        sj = [self.sb([128, 2056], F32, f"sj{i}") for i in range(2)]
        Ap = self.sb([128, 4], F32, "Ap")
        k = 0
        NCn = c.NC
        for d_, order in ((0, range(NCn)), (1, reversed(range(NCn)))):
            S_ = G.Sf if d_ == 0 else G.Sb
            for j in order:
                s = sj[k % 2]
                k += 1
                self.dma("sp", s.ap, self.summ_all.ap[j * 128:(j + 1) * 128, :], r=[self.summ_all.b], w=[s.b])
                m = self.cm.ap[:, (2 * d_) * NCn + j:(2 * d_) * NCn + j + 1]
                m1 = self.cm.ap[:, (2 * d_ + 1) * NCn + j:(2 * d_ + 1) * NCn + j + 1]
                self.ts("dve", Ap.ap, s.ap[:, 2048 + 4 * d_:2052 + 4 * d_], m, ALU.mult, m1, ALU.add,
                        r=[s.b, self.cm.b], w=[Ap.b])
                for h in range(4):
                    sl = S_.ap[:, h * 256:(h + 1) * 256]
                    self.ts("dve", sl, sl, Ap.ap[:, h:h + 1], ALU.mult, r=[S_.b, Ap.b], w=[S_.b])
                    self.stt(sl, s.ap[:, d_ * 1024 + h * 256:d_ * 1024 + (h + 1) * 256], m, sl, ALU.mult, ALU.add,
                             r=[s.b, self.cm.b, S_.b], w=[S_.b])
        for i in reversed(range(c.NT)):
            self.gla_tile(i * 128, "p2a", sidx=i)
        for i in range(c.NT):
            self.gla_tile(i * 128, "p2b", sidx=i, xdst=self.xT_d, xcol=i * 128, gate=gate_x, gdep=modx.b)
        self.phase_reset()
        self.halo_prologue(0)
        self.phase_reset()

    def halo_prologue(self, l):
        c = self.cfg
        A, sh, _, mod = self.scal("ffn", l, "x")
        hb = self.sb([128, 8, 2], BF16, "halo_h")
        self.modnorm(self.xT_d, 0, 1, A, sh, [mod.b], hb, 0)
        self.modnorm(self.xT_d, c.T - 1, 1, A, sh, [mod.b], hb, 1)
        hf = self.sb([128, 16], F32, "halo_f")
        self.cp("dve", hf.ap, hb.ap.rearrange("p c t -> p (c t)"), r=[hb.b], w=[hf.b])
        self.dma("sp", self.halo_d[l].ap, hf.ap, r=[hf.b], w=[self.halo_d[l].b])

    def ffn_phase(self, l, with_ctx):
        c = self.cfg
        I = self.I
        NCn = c.NC
        hx = self.sb([128, 8, c.T + 2], BF16, "f_hx")
        hc = self.sb([128, 8, c.CTX + 2], BF16, "f_hc") if with_ctx else None
        accL = self.sb([128, 8], F32, "f_accL")
        accR = self.sb([128, 8], F32, "f_accR")
        hal = self.sb([128, NCn, 16], F32, "f_hal")
        A, sh, gate_x, modx = self.scal("ffn", l, "x")
        mark = self.modnorm(self.xT_d, 0, c.T, A, sh, [modx.b], hx, 1)
        if with_ctx:
            self.memset("pool", hc.ap, 0.0, w=[hc.b])
            Ac, shc, gate_c, modc = self.scal("ffn", l, "c")
            self.modnorm(self.cT_d, 0, c.CTX, Ac, shc, [modc.b], hc, 1)
        self.dma("sp", hal.ap, self.halo_all[l].ap.rearrange("(r p) n -> p r n", p=128),
                 r=[self.halo_all[l].b], w=[hal.b])
        hv = hal.ap.rearrange("p r (c t) -> p r c t", t=2)
        for acc, t_, mrow in ((accL, 1, 4), (accR, 0, 5)):
            for j in range(NCn):
                m = self.cm.ap[:, mrow * NCn + j:mrow * NCn + j + 1]
                if j == 0:
                    self.ts("dve", acc.ap, hv[:, j, :, t_], m, ALU.mult, r=[hal.b, self.cm.b], w=[acc.b])
                else:
                    self.stt(acc.ap, hv[:, j, :, t_], m, acc.ap, ALU.mult, ALU.add,
                             r=[hal.b, self.cm.b, acc.b], w=[acc.b])
        self.cp("dve", hx.ap[:, :, 0], accL.ap, r=[accL.b, hx.b], w=[hx.b])
        self.cp("dve", hx.ap[:, :, c.T + 1], accR.ap, r=[accR.b, hx.b], w=[hx.b])
        self.free_to(mark)
        FC = c.FC
        NG = min(c.NG, FC)
        bounds = [round(i * FC / NG) for i in range(NG + 1)]
        groups = [list(range(bounds[i], bounds[i + 1])) for i in range(NG)]
        gmax = max(len(g) for g in groups)
        wa = [self.sb([128, 8, gmax * 128], BF16, f"f_wa{i}") for i in range(2)]
        wb = [self.sb([128, 8, gmax * 128], BF16, f"f_wb{i}") for i in range(2)]
        wd = [self.sb([128, gmax, D], BF16, f"f_wd{i}") for i in range(2)]
        ta = [self.sb([128, 256], F32, f"f_ta{i}") for i in range(2)]
        tb = [self.sb([128, 256], F32, f"f_tb{i}") for i in range(2)]
        sa = [self.sb([128, 256], F32, f"f_sa{i}") for i in range(2)]
        gT = [self.sb([128, 256], BF16, f"f_g{i}") for i in range(2)]
        xblk = [self.sb([128, 8, 256], F32, f"f_x{i}") for i in range(2)]
        blocks = [(hx, self.xT_d, gate_x, modx, o, 256) for o in range(0, c.T, 256)]
        if with_ctx:
            blocks += [(hc, self.cT_d, gate_c, modc, o, min(256, c.CTX - o)) for o in range(0, c.CTX, 256)]
        cw = self.cw[l]
        P = self.pb
        wup = I["ffn_wup"].ap[l].rearrange("(c p) n -> p c n", p=128)
        wdn = I["ffn_wdown"].ap[l].rearrange("(j p) d -> p j d", p=128)
        cnt = 0
        xk = 0
        for gi, grp in enumerate(groups):
            s = gi % 2
            g = len(grp)
            j0 = grp[0]
            self.dma("pool", wa[s].ap[:, :, 0:g * 128], wup[:, :, j0 * 128:(j0 + g) * 128], w=[wa[s].b])
            self.dma("pool", wb[s].ap[:, :, 0:g * 128],
                     wup[:, :, c.DFF + j0 * 128:c.DFF + (j0 + g) * 128], w=[wb[s].b])
            self.dma("pool", wd[s].ap[:, 0:g, :], wdn[:, j0:j0 + g, :], w=[wd[s].b])
            for (h_, xd, gate, mod, o, n) in blocks:
                for jj, j in enumerate(grp):
                    u = cnt % 2
                    cnt += 1
                    ua, ub = P[4 + 2 * u], P[5 + 2 * u]
                    for (U, W) in ((ua, wa[s]), (ub, wb[s])):
                        for cc in range(8):
                            self.mm(U.ap[:, 0:n + 2], W.ap[:, cc, jj * 128:(jj + 1) * 128],
                                    h_.ap[:, cc, o:o + n + 2], start=(cc == 0), stop=(cc == 7),
                                    r=[W.b, h_.b], w=[U.b])
                    for (U, t_, fj) in ((ua, ta[u], j), (ub, tb[u], FC + j)):
                        self.act(t_.ap[:, 0:n], U.ap[:, 0:n], AF.Identity, r=[U.b, cw[0].b, cw[3].b], w=[t_.b],
                                 scale=cw[0].ap[:, fj:fj + 1], bias=cw[3].ap[:, fj:fj + 1])
                        self.stt(t_.ap[:, 0:n], U.ap[:, 1:n + 1], cw[1].ap[:, fj:fj + 1], t_.ap[:, 0:n],
                                 ALU.mult, ALU.add, r=[U.b, t_.b, cw[1].b], w=[t_.b])
                        self.stt(t_.ap[:, 0:n], U.ap[:, 2:n + 2], cw[2].ap[:, fj:fj + 1], t_.ap[:, 0:n],
                                 ALU.mult, ALU.add, r=[U.b, t_.b, cw[2].b], w=[t_.b])
                    self.act(sa[u].ap[:, 0:n], ta[u].ap[:, 0:n], AF.Silu, r=[ta[u].b], w=[sa[u].b])
                    self.tt("pool", gT[u].ap[:, 0:n], sa[u].ap[:, 0:n], tb[u].ap[:, 0:n], ALU.mult,
                            r=[sa[u].b, tb[u].b], w=[gT[u].b])
                    for dc in range(8):
                        OB = P[dc // 2]
                        self.mm(OB.ap[:, (dc % 2) * 256:(dc % 2) * 256 + n], wd[s].ap[:, jj, dc * 128:(dc + 1) * 128],
                                gT[u].ap[:, 0:n], start=(jj == 0 and dc % 2 == 0), stop=(jj == g - 1),
                                r=[gT[u].b, wd[s].b], w=[OB.b], skip=True)
                xb = xblk[xk % 2]
                xk += 1
                xv = xd.ap.rearrange("(c p) t -> p c t", p=128)[:, :, o:o + n]
                self.dma("sp", xb.ap[:, :, 0:n], xv, r=[xd.b], w=[xb.b])
                for dc in range(8):
                    OB = P[dc // 2]
                    self.stt(xb.ap[:, dc, 0:n], OB.ap[:, (dc % 2) * 256:(dc % 2) * 256 + n], gate[:, dc:dc + 1],
                             xb.ap[:, dc, 0:n], ALU.mult, ALU.add, r=[OB.b, xb.b, mod.b], w=[xb.b])
                self.dma("sp", xv, xb.ap[:, :, 0:n], r=[xb.b], w=[xd.b])


    def phase_C(self):
        self.ffn_phase(0, True)
        self.phase_reset()
        self.l1_proj()
        self.phase_reset()

    def l1_proj(self):
        c = self.cfg
        I = self.I
        P = self.pb
        ntot = c.T + c.CTX
        hT = self.sb([128, 8, ntot], BF16, "p_hT")
        A, sh, _, mod = self.scal("mix", 1, "x")
        mark = self.modnorm(self.xT_d, 0, c.T, A, sh, [mod.b], hT, 0)
        A, sh, _, mod = self.scal("mix", 1, "c")
        self.modnorm(self.cT_d, 0, c.CTX, A, sh, [mod.b], hT, c.T)
        self.free_to(mark)

        def wload(name, src3):
            w = self.sb([128, 8, D], BF16, name)
            self.dma("pool", w.ap, src3.rearrange("(c p) n -> p c n", p=128), w=[w.b])
            return w
        wq = wload("p_wq", I["diff_wq"].ap[0])
        wk = wload("p_wk", I["diff_wk"].ap[0])
        wv = wload("p_wv", I["diff_wv"].ap[0])
        rm = self.sb([128, 128], F32, "p_rm")
        self.dma("sp", rm.ap, I["k_rm"].ap, w=[rm.b])
        cos = self.sb([128, c.T], F32, "p_cos")
        sin = self.sb([128, c.T], F32, "p_sin")
        self.dma("sp", cos.ap, I["k_cos"].ap, w=[cos.b])
        self.dma("sp", sin.ap, I["k_sin"].ap, w=[sin.b])
        BLK = 512 if c.T % 512 == 0 else 256
        qf = [self.sb([128, BLK], F32, f"p_qf{i}") for i in range(2)]
        t1 = [self.sb([128, BLK], F32, f"p_t1{i}") for i in range(2)]
        t2 = [self.sb([128, BLK], F32, f"p_t2{i}") for i in range(2)]
        ob = [self.sb([128, BLK], BF16, f"p_ob{i}") for i in range(2)]
        k = 0
        for t0 in range(0, c.T, BLK):
            n = BLK
            for (W, dst) in ((wq, self.qT_d), (wk, self.kv_loc)):
                for h in range(8):
                    u = k % 2
                    k += 1
                    Ba, Bb = P[2 * u], P[2 * u + 1]
                    for cc in range(8):
                        self.mm(Ba.ap[:, 0:n], W.ap[:, cc, h * 128:(h + 1) * 128], hT.ap[:, cc, t0:t0 + n],
                                start=(cc == 0), stop=(cc == 7), r=[W.b, hT.b], w=[Ba.b])
                    self.cp("act", qf[u].ap[:, 0:n], Ba.ap[:, 0:n], r=[Ba.b], w=[qf[u].b])
                    self.mm(Bb.ap[:, 0:n], rm.ap, qf[u].ap[:, 0:n], r=[rm.b, qf[u].b], w=[Bb.b])
                    self.tt("pool", t1[u].ap[:, 0:n], qf[u].ap[:, 0:n], cos.ap[:, t0:t0 + n], ALU.mult,
                            r=[qf[u].b, cos.b], w=[t1[u].b])
                    self.tt("dve", t2[u].ap[:, 0:n], Bb.ap[:, 0:n], sin.ap[:, t0:t0 + n], ALU.mult,
                            r=[Bb.b, sin.b], w=[t2[u].b])
                    self.tt("dve", ob[u].ap[:, 0:n], t1[u].ap[:, 0:n], t2[u].ap[:, 0:n], ALU.add,
                            r=[t1[u].b, t2[u].b], w=[ob[u].b])
                    self.dma("sp", dst.ap[h * 128:(h + 1) * 128, t0:t0 + n], ob[u].ap[:, 0:n],
                             r=[ob[u].b], w=[dst.b])
        vs = [self.sb([128, D], BF16, f"p_vs{i}") for i in range(2)]
        kc = [self.sb([128, c.CTX], BF16, f"p_kc{i}") for i in range(2)]

        def vtile(col, dstv, i, u):
            for half, Bv in enumerate((P[4 + 2 * u], P[5 + 2 * u])):
                for cc in range(8):
                    self.mm(Bv.ap, hT.ap[:, cc, col:col + 128], wv.ap[:, cc, half * 512:(half + 1) * 512],
                            start=(cc == 0), stop=(cc == 7), r=[wv.b, hT.b], w=[Bv.b])
                eng = "act" if half == 0 else "dve"
                self.cp(eng, vs[u].ap[:, half * 512:(half + 1) * 512], Bv.ap, r=[Bv.b, vs[u].b], w=[vs[u].b])
            self.dma("sp", dstv[:, :, i, :], vs[u].ap.rearrange("p (h v) -> p h v", h=8), r=[vs[u].b],
                     w=[self.kv_loc.b, self.kvc_d.b])
        vown = self.kv_loc.ap[1024:2048, :].rearrange("(h p) (kt v) -> p h kt v", p=128, v=128)
        for i in range(c.NT):
            vtile(i * 128, vown, i, i % 2)
        vctx = self.kvc_d.ap[1024:2048, :].rearrange("(h p) (kt v) -> p h kt v", p=128, v=128)
        for i in range(c.NCT):
            vtile(c.T + i * 128, vctx, i, i % 2)
        for h in range(8):
            u = h % 2
            Ba = P[u]
            for cc in range(8):
                self.mm(Ba.ap[:, 0:c.CTX], wk.ap[:, cc, h * 128:(h + 1) * 128], hT.ap[:, cc, c.T:c.T + c.CTX],
                        start=(cc == 0), stop=(cc == 7), r=[wk.b, hT.b], w=[Ba.b])
            self.cp("act", kc[u].ap, Ba.ap[:, 0:c.CTX], r=[Ba.b], w=[kc[u].b])
            self.dma("sp", self.kvc_d.ap[h * 128:(h + 1) * 128, :], kc[u].ap, r=[kc[u].b], w=[self.kvc_d.b])

    def phase_D(self):
        import math
        c = self.cfg
        I = self.I
        P = self.pb
        NKT = c.NCT + c.NC * c.NT
        NK = NKT * 128
        li = 0.8 - 0.6 * math.exp(-0.3 * 1)
        scale = HD ** -0.5
        lq = self.sb([1, 4, 64], F32, "d_lq")
        for i_, nme in enumerate(("diff_lq1", "diff_lk1", "diff_lq2", "diff_lk2")):
            self.dma("sp", lq.ap[:, i_, :], I[nme].ap, r=[lq.b], w=[lq.b])
        pr = self.sb([1, 2, 64], F32, "d_pr")
        lqv = lq.ap.rearrange("p (a b) n -> p a b n", b=2)
        self.tt("dve", pr.ap, lqv[:, :, 0, :], lqv[:, :, 1, :], ALU.mult, r=[lq.b], w=[pr.b])
        sm = self.sb([1, 2], F32, "d_sm")
        self.S.op("dve", lambda e: e.tensor_reduce(out=sm.ap, in_=pr.ap, axis=mybir.AxisListType.X, op=ALU.add),
                  reads=[pr.b], writes=[sm.b])
        ex = self.sb([1, 2], F32, "d_ex")
        self.act(ex.ap, sm.ap, AF.Exp, r=[sm.b], w=[ex.b])
        dd = self.sb([1, 1], F32, "d_dd")
        self.tt("dve", dd.ap, ex.ap[:, 0:1], ex.ap[:, 1:2], ALU.subtract, r=[ex.b], w=[dd.b])
        nl2 = self.sb([1, 2], F32, "d_nl2")
        for j in range(2):
            self.ts("dve", nl2.ap[:, j:j + 1], dd.ap, li, ALU.add, -1.0, ALU.mult, r=[dd.b, nl2.b], w=[nl2.b])
        neglam = self.sb([128, 2], F32, "d_neglam")
        self.mm(P[7].ap[:, 0:2], self.onesf.ap[0:1, :], nl2.ap, r=[self.onesf.b, nl2.b], w=[P[7].b])
        self.cp("dve", neglam.ap, P[7].ap[:, 0:2], r=[P[7].b], w=[neglam.b])
        srow = self.sb([1, 128], F32, "d_srow")
        self.dma("sp", srow.ap, I["diff_subln"].ap, w=[srow.b])
        gsub = self.sb([128, 128], F32, "d_gsub")
        self.mm(P[6].ap[:, 0:128], self.onesf.ap[0:1, :], srow.ap, r=[self.onesf.b, srow.b], w=[P[6].b])
        self.act(gsub.ap, P[6].ap[:, 0:128], AF.Copy, r=[P[6].b], w=[gsub.b], scale=(1.0 - li))
        OnT = self.sb([128, 8, c.T], BF16, "d_OnT")
        attn_mark = self.aoff
        KT = [self.sb([128, NK], BF16, f"d_KT{i}") for i in range(2)]
        V = [self.sb([128, NKT, 130], BF16, f"d_V{i}") for i in range(2)]
        for v_ in V:
            self.memset("pool", v_.ap, 1.0, w=[v_.b])
        qh = [self.sb([128, c.T], BF16, f"d_qh{i}") for i in range(2)]
        PT = [self.sb([128, 512], BF16, f"d_PT{i}") for i in range(3)]
        r1 = self.sb([128, 1], F32, "d_r1")
        r2 = self.sb([128, 1], F32, "d_r2")
        r2l = self.sb([128, 1], F32, "d_r2l")
        ss = self.sb([128, 1], F32, "d_ss")
        rt = self.sb([128, 1], F32, "d_rt")
        rstd = self.sb([128, 1], F32, "d_rstd")
        Ab = self.sb([128, 128], F32, "d_Ab")
        Ob = self.sb([128, 128], F32, "d_Ob")
        sqj = self.sb([128, 128], F32, "d_sqj")
        On = self.sb([128, 128], BF16, "d_On")
        it = 0
        for h in range(8):
            s = h % 2
            K_, V_, q_ = KT[s], V[s], qh[s]
            self.dma("sp", K_.ap[:, 0:c.CTX], self.kvc_d.ap[h * 128:(h + 1) * 128, :], r=[self.kvc_d.b], w=[K_.b])
            self.dma("sp", V_.ap[:, 0:c.NCT, 0:128],
                     self.kvc_d.ap[1024 + h * 128:1024 + (h + 1) * 128, :].rearrange("p (kt v) -> p kt v", v=128),
                     r=[self.kvc_d.b], w=[V_.b])
            for r_ in range(c.NC):
                base = r_ * 2048
                self.dma("sp", K_.ap[:, c.CTX + r_ * c.T:c.CTX + (r_ + 1) * c.T],
                         self.kv_all.ap[base + h * 128:base + (h + 1) * 128, :], r=[self.kv_all.b], w=[K_.b])
                self.dma("sp", V_.ap[:, c.NCT + r_ * c.NT:c.NCT + (r_ + 1) * c.NT, 0:128],
                         self.kv_all.ap[base + 1024 + h * 128:base + 1024 + (h + 1) * 128, :]
                         .rearrange("p (kt v) -> p kt v", v=128), r=[self.kv_all.b], w=[V_.b])
            self.dma("sp", q_.ap, self.qT_d.ap[h * 128:(h + 1) * 128, :], r=[self.qT_d.b], w=[q_.b])
            for qb in range(c.T // 256):
                q0 = qb * 256
                for kt in range(NKT):
                    p_ = it % 2
                    pt = PT[it % 3]
                    it += 1
                    S1, S2 = P[4 + 2 * p_], P[5 + 2 * p_]
                    self.mm(S1.ap[:, 0:256], K_.ap[0:64, kt * 128:(kt + 1) * 128], q_.ap[0:64, q0:q0 + 256],
                            r=[K_.b, q_.b], w=[S1.b])
                    self.mm(S2.ap[:, 0:256], K_.ap[64:128, kt * 128:(kt + 1) * 128], q_.ap[64:128, q0:q0 + 256],
                            r=[K_.b, q_.b], w=[S2.b])
                    sv = self.psum[:, (4 + 2 * p_) * 512:(4 + 2 * p_) * 512 + 1024].rearrange(
                        "p (a n) -> p a n", a=2)[:, :, 0:256]
                    self.act(pt.ap.rearrange("p (a n) -> p a n", a=2), sv, AF.Exp, r=[S1.b, S2.b], w=[pt.b],
                             scale=scale)
                    for sb_ in range(2):
                        for n_ in range(2):
                            acc = P[sb_ * 2 + n_]
                            self.mm(acc.ap[:, 0:129], pt.ap[:, n_ * 256 + sb_ * 128:n_ * 256 + sb_ * 128 + 128],
                                    V_.ap[:, kt, 0:129], start=(kt == 0), stop=(kt == NKT - 1),
                                    r=[pt.b, V_.b], w=[acc.b])
                for sb_ in range(2):
                    A1, A2 = P[sb_ * 2], P[sb_ * 2 + 1]
                    self.recip(r1.ap, A1.ap[:, 128:129], r=[A1.b], w=[r1.b])
                    self.recip(r2.ap, A2.ap[:, 128:129], r=[A2.b], w=[r2.b])
                    self.tt("dve", r2l.ap, r2.ap, neglam.ap[:, 0:1], ALU.mult, r=[r2.b, neglam.b], w=[r2l.b])
                    self.ts("dve", Ab.ap, A1.ap[:, 0:128], r1.ap[:, 0:1], ALU.mult, r=[A1.b, r1.b], w=[Ab.b])
                    self.stt(Ob.ap, A2.ap[:, 0:128], r2l.ap[:, 0:1], Ab.ap, ALU.mult, ALU.add,
                             r=[A2.b, r2l.b, Ab.b], w=[Ob.b])
                    self.act(sqj.ap, Ob.ap, AF.Square, r=[Ob.b], w=[sqj.b, ss.b], accum=ss.ap)
                    self.act(rt.ap, ss.ap, AF.Sqrt, r=[ss.b], w=[rt.b], scale=1.0 / 128.0, bias=self.epsb.ap)
                    self.recip(rstd.ap, rt.ap, r=[rt.b], w=[rstd.b])
                    self.stt(On.ap, Ob.ap, rstd.ap[:, 0:1], gsub.ap, ALU.mult, ALU.mult,
                             r=[Ob.b, rstd.b, gsub.b], w=[On.b])
                    trb = P[4].ap.bitcast(BF16)[:, 0:128]
                    self.tr(trb, On.ap, self.identb.ap, r=[On.b, self.identb.b], w=[P[4].b])
                    self.cp("act", OnT.ap[:, h, q0 + sb_ * 128:q0 + (sb_ + 1) * 128], trb, r=[P[4].b, OnT.b],
                            w=[OnT.b])
        self.free_to(attn_mark)
        dwo = self.sb([128, 8, D], BF16, "d_wo")
        self.dma("pool", dwo.ap, I["diff_wo"].ap[0].rearrange("(h p) n -> p h n", p=128), w=[dwo.b])
        _, _, gate, mod = self.scal("mix", 1, "x")
        xblk = [self.sb([128, 8, 256], F32, f"d_x{i}") for i in range(2)]
        for bi, o in enumerate(range(0, c.T, 256)):
            for dc in range(8):
                OB = P[dc // 2]
                for h in range(8):
                    self.mm(OB.ap[:, (dc % 2) * 256:(dc % 2) * 256 + 256], dwo.ap[:, h, dc * 128:(dc + 1) * 128],
                            OnT.ap[:, h, o:o + 256], start=(h == 0), stop=(h == 7), r=[dwo.b, OnT.b], w=[OB.b])
            xb = xblk[bi % 2]
            xv = self.xT_d.ap.rearrange("(c p) t -> p c t", p=128)[:, :, o:o + 256]
            self.dma("sp", xb.ap, xv, r=[self.xT_d.b], w=[xb.b])
            for dc in range(8):
                OB = P[dc // 2]
                self.stt(xb.ap[:, dc, :], OB.ap[:, (dc % 2) * 256:(dc % 2) * 256 + 256], gate[:, dc:dc + 1],
                         xb.ap[:, dc, :], ALU.mult, ALU.add, r=[OB.b, xb.b, mod.b], w=[xb.b])
            self.dma("sp", xv, xb.ap, r=[xb.b], w=[self.xT_d.b])
        self.phase_reset()
        self.halo_prologue(1)
        self.phase_reset()

    def phase_E(self):
        c = self.cfg
        P = self.pb
        self.ffn_phase(1, False)
        self.phase_reset()
        BLK = 512 if c.T % 512 == 0 else 256
        yT = [self.sb([128, 8, BLK], F32, f"e_y{i}") for i in range(2)]
        ysb = [self.sb([128, D], F32, f"e_o{i}") for i in range(2)]
        k = 0
        for bi, t0 in enumerate(range(0, c.T, BLK)):
            y = yT[bi % 2]
            self.modnorm(self.xT_d, t0, BLK, self.gfin, None, [], None, 0, out_f32=TT(y.ap[:, :, :], y.b))
            for ti in range(BLK // 128):
                o_ = ysb[k % 2]
                for half in range(2):
                    bank = P[4 + (2 * k + half) % 4]
                    for cc in range(4):
                        self.tr(bank.ap[:, cc * 128:(cc + 1) * 128], y.ap[:, half * 4 + cc, ti * 128:(ti + 1) * 128],
                                self.ident.ap, r=[y.b, self.ident.b], w=[bank.b])
                    eng = "act" if half == 0 else "dve"
                    self.cp(eng, o_.ap[:, half * 512:(half + 1) * 512], bank.ap, r=[bank.b, o_.b], w=[o_.b])
                k += 1
                self.dma("sp", self.out.ap[t0 + ti * 128:t0 + (ti + 1) * 128, :], o_.ap, r=[o_.b], w=[self.out.b])


def host_consts(cfg, core):
    NC, T = cfg.NC, cfg.T
    k = {}
    k["k_ident"] = np.eye(128, dtype=np.float32)
    s = np.arange(128)[:, None]
    t = np.arange(128)[None, :]
    k["k_ltf"] = np.where(s <= t, -1.0 / 16.0, 0.0).astype(np.float32)
    k["k_ltb"] = np.where(s >= t, -1.0 / 16.0, 0.0).astype(np.float32)
    k["k_mf"] = np.tile((s <= t).astype(np.float32), (1, 4))
    k["k_mb"] = np.tile((s >= t).astype(np.float32), (1, 4))
    rm = np.zeros((128, 128), np.float32)
    for n in range(2):
        for a in range(2):
            for f in range(16):
                lo = n * 64 + a * 32 + f
                hi = lo + 16
                rm[hi, lo] = -1.0
                rm[lo, hi] = 1.0
    k["k_rm"] = rm
    pos = core * T + np.arange(T)
    row = (pos // GRID_W).astype(np.float32)
    col = (pos % GRID_W).astype(np.float32)
    quarter = HD // 4
    inv = (np.float32(ROPE_BASE) ** (-np.arange(quarter, dtype=np.float32) / np.float32(quarter))).astype(np.float32)
    ang_r = row[:, None] * inv
    ang_c = col[:, None] * inv
    ang = np.concatenate([ang_r, ang_r, ang_c, ang_c], axis=-1)
    cos = np.cos(ang).astype(np.float32).T
    sin = np.sin(ang).astype(np.float32).T
    k["k_cos"] = np.ascontiguousarray(np.concatenate([cos, cos], axis=0))
    k["k_sin"] = np.ascontiguousarray(np.concatenate([sin, sin], axis=0))
    j = np.arange(NC)
    mf = (j < core).astype(np.float32)
    mb = (j > core).astype(np.float32)
    ol = (j == core - 1).astype(np.float32)
    orr = (j == core + 1).astype(np.float32)
    cm = np.concatenate([mf, 1 - mf, mb, 1 - mb, ol, orr])[None, :]
    k["k_cm"] = np.ascontiguousarray(np.repeat(cm, 128, axis=0).astype(np.float32))
    k["k_n16"] = np.full((128, 2), -1.0 / 16.0, np.float32)
    return k


def build(cfg, phases=("A", "B", "C", "D", "E"), fused=True, debug=(), imports=(), exports=()):
    kb = KB(cfg, phases, fused, debug)
    with kb.st:
        kb.declare_inputs()
        kb.declare_scratch(imports, exports)
        if "E" in phases:
            kb.out = TT(kb.nc.dram_tensor("out", [cfg.T, D], F32, kind="ExternalOutput").ap(), kb.S.buf("out"))
            kb.out_names.append("out")
        kb.setup()
        for ph in phases:
            getattr(kb, "phase_" + ph)()
            if kb.fused:
                if ph == "A":
                    kb.exchange(kb.summ_d, kb.summ_all)
                elif ph == "B":
                    kb.exchange(kb.halo_d[0], kb.halo_all[0])
                elif ph == "C":
                    kb.exchange(kb.kv_loc, kb.kv_all)
                elif ph == "D":
                    kb.exchange(kb.halo_d[1], kb.halo_all[1])
        kb.S.barrier()
        kb.S.emit(kb.st)
    return kb


def make_in_maps(cfg, inputs):
    NC, T = cfg.NC, cfg.T
    f = lambda a: np.ascontiguousarray(np.asarray(a, dtype=np.float32))
    shared = {}
    for kname, v in inputs.items():
        if kname == "x":
            continue
        a = f(v)
        if kname == "ctx":
            a = a[0]
        elif kname in ("c_ctx", "final_norm"):
            a = a.reshape(1, -1)
        shared[kname] = a
    maps = []
    x = f(inputs["x"])[0]
    for r in range(NC):
        m = dict(shared)
        m["x"] = np.ascontiguousarray(x[r * T:(r + 1) * T])
        m.update(host_consts(cfg, r))
        maps.append(m)
    return maps


_CACHE = {}


def kernel(**inputs):
    cfg = Cfg()
    if "kb" not in _CACHE:
        _CACHE["kb"] = build(cfg)
    kb = _CACHE["kb"]
    maps = make_in_maps(cfg, inputs)
    res = run_bass_kernel_spmd(kb.nc, maps, core_ids=list(range(cfg.NC)))
    out = np.concatenate([np.asarray(res.results[r]["out"], dtype=np.float32) for r in range(cfg.NC)], axis=0)
    return out.reshape(1, cfg.NC * cfg.T, D)
```
